# Optimizing a Trainium2 kernel written in Bass

```python
import math
import jax, jax.numpy as jnp
from jax import lax
import numpy as np

D_MODEL = 1024
BATCH = 8
SEQ = 4096
DEPTH = 4

HEAD_DIM = 64
N_HEADS = 8
N_KV_HEADS = 2
WINDOW = 128
ATTN_WIDTH = N_HEADS * HEAD_DIM
KV_WIDTH = N_KV_HEADS * HEAD_DIM
CONV_CH = D_MODEL // 2
CONV_WIDTH = 31
MIX_WIDTH = ATTN_WIDTH + CONV_CH
IN_WIDTH = ATTN_WIDTH + 2 * KV_WIDTH + 2 * CONV_CH
D_FF = 2816
FFN_RESIDUAL_WEIGHT = 0.5
EPS = 1e-6
NEG_INF = -1e30

kernel_name = "hybrid_swa_sink_alibi_conformer_conv_macaron"


def rms_norm(x, g):
    xf = x.astype(jnp.float32)
    y = xf * lax.rsqrt(jnp.mean(xf * xf, axis=-1, keepdims=True) + EPS)
    return (y * g.astype(jnp.float32)).astype(x.dtype)


def swiglu_ffn(h, w_in, w_out):
    gu = h @ w_in
    gate, up = jnp.split(gu, 2, axis=-1)
    return (jax.nn.silu(gate) * up) @ w_out


def alibi_slopes(n_heads):
    return jnp.exp2(-8.0 * jnp.arange(1, n_heads + 1, dtype=jnp.float32) / n_heads)


def sliding_window_sink_attention(q, k, v, sinks):
    B, S, H, hd = q.shape
    nb = S // WINDOW
    G = H // N_KV_HEADS
    qb = q.reshape(B, nb, WINDOW, N_KV_HEADS, G, hd).astype(jnp.float32)

    def band(t):
        cur = t.reshape(B, nb, WINDOW, N_KV_HEADS, hd)
        prev = jnp.pad(cur, ((0, 0), (1, 0), (0, 0), (0, 0), (0, 0)))[:, :-1]
        return jnp.concatenate([prev, cur], axis=2).astype(jnp.float32)

    kb, vb = band(k), band(v)
    scores = jnp.einsum('bnqkgd,bnskd->bkgnqs', qb, kb) * (1.0 / math.sqrt(hd))

    t_loc = jnp.arange(WINDOW)[:, None]
    s_loc = jnp.arange(2 * WINDOW)[None, :]
    dist = t_loc + WINDOW - s_loc
    in_window = (dist >= 0) & (dist < WINDOW)
    blk = jnp.arange(nb)[:, None, None]
    valid = in_window[None] & ((blk > 0) | (s_loc >= WINDOW)[None])

    slopes = alibi_slopes(H).reshape(N_KV_HEADS, G)
    bias = -slopes[:, :, None, None] * jnp.abs(dist).astype(jnp.float32)[None, None]
    scores = jnp.where(valid[None, None, None], scores + bias[:, :, None], NEG_INF)

    sink = sinks.astype(jnp.float32).reshape(N_KV_HEADS, G)[None, :, :, None, None]
    m = jnp.maximum(jnp.max(scores, axis=-1), sink)
    p = jnp.exp(scores - m[..., None])
    denom = jnp.sum(p, axis=-1) + jnp.exp(sink - m)
    p = p / denom[..., None]
    out = jnp.einsum('bkgnqs,bnskd->bnqkgd', p, vb)
    return out.reshape(B, S, H * hd).astype(q.dtype)


def conformer_conv(u, w_dw, b_dw, ln_g, ln_b):
    a, gate = jnp.split(u, 2, axis=-1)
    z = a * jax.nn.sigmoid(gate)
    C = z.shape[-1]
    y = lax.conv_general_dilated(
        z, w_dw.astype(z.dtype)[:, None, :],
        window_strides=(1,), padding=[(CONV_WIDTH - 1, 0)],
        dimension_numbers=('NWC', 'WIO', 'NWC'), feature_group_count=C)
    y = (y + b_dw).astype(jnp.float32)
    mu = jnp.mean(y, axis=-1, keepdims=True)
    var = jnp.mean(jnp.square(y - mu), axis=-1, keepdims=True)
    y = (y - mu) * lax.rsqrt(var + EPS) * ln_g.astype(jnp.float32) + ln_b.astype(jnp.float32)
    return jax.nn.silu(y).astype(u.dtype)


def setup_inputs(seed: int = 0) -> dict:
    key = jax.random.key(seed)
    ks = jax.random.split(key, 20)
    f32 = jnp.float32

    def nrm(k, shape, scale):
        return jax.random.normal(k, shape, f32) * scale

    def gain(k, shape):
        return 1.0 + 0.05 * jax.random.normal(k, shape, f32)

    return {
        "x": jax.random.normal(ks[0], (BATCH, SEQ, D_MODEL), f32),
        "norm_ffn1": gain(ks[1], (DEPTH, D_MODEL)),
        "w_ffn1_in": nrm(ks[2], (DEPTH, D_MODEL, 2 * D_FF), D_MODEL ** -0.5),
        "w_ffn1_out": nrm(ks[3], (DEPTH, D_FF, D_MODEL), D_FF ** -0.5),
        "norm_mix": gain(ks[4], (DEPTH, D_MODEL)),
        "w_in": nrm(ks[5], (DEPTH, D_MODEL, IN_WIDTH), D_MODEL ** -0.5),
        "sinks": nrm(ks[6], (DEPTH, N_HEADS), 1.0),
        "w_dw": nrm(ks[7], (DEPTH, CONV_WIDTH, CONV_CH), CONV_WIDTH ** -0.5),
        "b_dw": nrm(ks[8], (DEPTH, CONV_CH), 0.02),
        "conv_ln_g": gain(ks[9], (DEPTH, CONV_CH)),
        "conv_ln_b": nrm(ks[10], (DEPTH, CONV_CH), 0.02),
        "w_out": nrm(ks[11], (DEPTH, MIX_WIDTH, D_MODEL), MIX_WIDTH ** -0.5),
        "norm_ffn2": gain(ks[12], (DEPTH, D_MODEL)),
        "w_ffn2_in": nrm(ks[13], (DEPTH, D_MODEL, 2 * D_FF), D_MODEL ** -0.5),
        "w_ffn2_out": nrm(ks[14], (DEPTH, D_FF, D_MODEL), D_FF ** -0.5),
        "final_norm": gain(ks[15], (D_MODEL,)),
    }


def reference(x, norm_ffn1, w_ffn1_in, w_ffn1_out, norm_mix, w_in, sinks, w_dw, b_dw,
              conv_ln_g, conv_ln_b, w_out, norm_ffn2, w_ffn2_in, w_ffn2_out, final_norm):
    B, S, _ = x.shape
    split_pts = [ATTN_WIDTH, ATTN_WIDTH + KV_WIDTH, ATTN_WIDTH + 2 * KV_WIDTH]
    for l in range(DEPTH):
        h = rms_norm(x, norm_ffn1[l])
        x = x + FFN_RESIDUAL_WEIGHT * swiglu_ffn(h, w_ffn1_in[l], w_ffn1_out[l])

        h = rms_norm(x, norm_mix[l])
        proj = h @ w_in[l]
        q, k, v, u = jnp.split(proj, split_pts, axis=-1)
        attn = sliding_window_sink_attention(
            q.reshape(B, S, N_HEADS, HEAD_DIM),
            k.reshape(B, S, N_KV_HEADS, HEAD_DIM),
            v.reshape(B, S, N_KV_HEADS, HEAD_DIM),
            sinks[l])
        conv = conformer_conv(u, w_dw[l], b_dw[l], conv_ln_g[l], conv_ln_b[l])
        x = x + jnp.concatenate([attn, conv], axis=-1) @ w_out[l]

        h = rms_norm(x, norm_ffn2[l])
        x = x + FFN_RESIDUAL_WEIGHT * swiglu_ffn(h, w_ffn2_in[l], w_ffn2_out[l])
    return rms_norm(x, final_norm)
```

```python
import numpy as np
from contextlib import ExitStack
import concourse.bass as bass
import concourse.mybir as mybir
from concourse.bass_utils import run_bass_kernel_spmd

F32 = mybir.dt.float32
BF16 = mybir.dt.bfloat16
AF = mybir.ActivationFunctionType
ALU = mybir.AluOpType

D = 1024
NCH = 8
DFF = 2816
NR = 11
INW = 1792
NH = 8
HD = 64
CW = 31
EPS = 1e-6
P = 128
SUB = 512
GPERM = (0, 2, 1, 3)


class Buf:
    __slots__ = ("w", "r", "const")

    def __init__(self, const=False):
        self.w = None
        self.r = []
        self.const = const


class Prog:
    ENGS = ("pe", "act", "dve", "pool", "sp")

    def __init__(self, nc, es):
        self.nc = nc
        self.es = es
        self.streams = {e: [] for e in self.ENGS}
        self.sems = {}
        self.count = {}
        self.waited = {e: {} for e in self.ENGS}
        self.fence_deps = {}
        for e in ("pe", "act", "dve", "pool"):
            self.new_sem("c_" + e)

    def fence(self):
        self.fence_deps = {"c_" + e: self.count["c_" + e] for e in ("pe", "act", "dve", "pool") if self.count["c_" + e] > 0}

    def new_sem(self, name):
        self.sems[name] = self.es.enter_context(self.nc.semaphore(name))
        self.count[name] = 0
        return name

    def op(self, eng, fns, reads=(), writes=(), dma_sem=None, nofence=False):
        if not isinstance(fns, (list, tuple)):
            fns = [fns]
        deps = {} if nofence else dict(self.fence_deps)

        def add(tok):
            if tok is not None and deps.get(tok[0], 0) < tok[1]:
                deps[tok[0]] = tok[1]

        for b in reads:
            add(b.w)
        for b in writes:
            add(b.w)
            for r in b.r:
                add(r)
        st = self.streams[eng]
        wd = self.waited[eng]
        for sname, val in deps.items():
            if eng == "pe" and sname == "c_pe":
                continue
            if wd.get(sname, 0) >= val:
                continue
            wd[sname] = val
            st.append(("w", sname, val))
        if dma_sem is not None:
            sname = dma_sem
            self.count[sname] += 16 * len(fns)
            tok = (sname, self.count[sname])
            st.append(("d", fns, sname))
        else:
            sname = "c_" + eng
            self.count[sname] += 1
            tok = (sname, self.count[sname])
            st.append(("o", fns, sname))
        for b in reads:
            if not b.const:
                b.r.append(tok)
        for b in writes:
            b.w = tok
            b.r = []
        return tok

    def final_wait(self, eng, toks):
        for tok in toks:
            self.streams[eng].append(("w", tok[0], tok[1]))

    def emit(self):
        nc = self.nc
        block = self.es.enter_context(nc.Block())
        sems = self.sems

        def run(stream):
            def f(e):
                for it in stream:
                    if it[0] == "w":
                        e.wait_ge(sems[it[1]], it[2])
                    elif it[0] == "d":
                        for fn in it[1]:
                            fn(e).then_inc(sems[it[2]], 16)
                    else:
                        fns = it[1]
                        for fn in fns[:-1]:
                            fn(e)
                        fns[-1](e).then_inc(sems[it[2]], 1)
            return f

        block.tensor(run(self.streams["pe"]))
        block.scalar(run(self.streams["act"]))
        block.vector(run(self.streams["dve"]))
        block.gpsimd(run(self.streams["pool"]))
        block.sync(run(self.streams["sp"]))


class Arena:
    def __init__(self, nc, nbytes):
        self.t = nc.alloc_sbuf_tensor("arena", [P, nbytes // 2], BF16)
        self.off = 0
        self.hi = 0
        self.cap = nbytes

    def alloc(self, free_shape, dt):
        esz = 4 if dt == F32 else 2
        n = 1
        for s in free_shape:
            n *= s
        nb = (n * esz + 63) // 64 * 64
        a = self.off
        self.off += nb
        self.hi = max(self.hi, self.off)
        assert self.off <= self.cap, (self.off, self.cap)
        ap = self.t[:, a // 2:(a + n * esz) // 2]
        if dt == F32:
            ap = ap.bitcast(F32)
        if len(free_shape) == 2:
            ap = ap.rearrange("p (a b) -> p a b", a=free_shape[0])
        elif len(free_shape) == 3:
            ap = ap.rearrange("p (a b c) -> p a b c", a=free_shape[0], b=free_shape[1])
        return ap


def build_program(DEPTH, S, TF=1024):
    NBLK = S // P
    TM = SUB
    NTM = S // TM
    TF = min(TF, S)
    NTF = S // TF
    NSUB = TF // SUB
    nc = bass.Bass("TRN2", target_bir_lowering=False)

    def din(name, shape):
        return nc.dram_tensor(name, list(shape), F32, kind="ExternalInput").ap()

    x_d = din("x", [S, D])
    nf1_d = din("norm_ffn1", [DEPTH, D])
    w1i_d = din("w_ffn1_in", [DEPTH, D, 2 * DFF])
    w1o_d = din("w_ffn1_out", [DEPTH, DFF, D])
    nmx_d = din("norm_mix", [DEPTH, D])
    win_d = din("w_in", [DEPTH, D, INW])
    sinks_d = din("sinks", [DEPTH, NH])
    wdw_d = din("w_dw", [DEPTH, CW, 512])
    bdw_d = din("b_dw", [DEPTH, 512])
    lng_d = din("conv_ln_g", [DEPTH, 512])
    lnb_d = din("conv_ln_b", [DEPTH, 512])
    wout_d = din("w_out", [DEPTH, D, D])
    nf2_d = din("norm_ffn2", [DEPTH, D])
    w2i_d = din("w_ffn2_in", [DEPTH, D, 2 * DFF])
    w2o_d = din("w_ffn2_out", [DEPTH, DFF, D])
    fn_d = din("final_norm", [D])
    bias8_d = din("bias8", [2, 2, P, 512])
    out_d = nc.dram_tensor("out", [S, D], F32, kind="ExternalOutput").ap()

    es = ExitStack()
    pg = Prog(nc, es)
    op = pg.op

    ar = Arena(nc, 212736)
    xT = ar.alloc([NCH, S], F32)
    ident_f = ar.alloc([P], F32)
    ident_b = ar.alloc([P], BF16)
    ones_b = ar.alloc([P], BF16)
    gT = ar.alloc([NCH, 16], F32)
    bias_sb = ar.alloc([4, 512], BF16)
    esink = ar.alloc([DEPTH * NH], F32)
    eps_t = ar.alloc([1], F32)
    dummy_t = ar.alloc([1], F32)
    wi = [ar.alloc([NCH, 512], BF16) for _ in range(2)]
    wo = [ar.alloc([2, D], BF16) for _ in range(2)]
    sq = [ar.alloc([SUB], BF16) for _ in range(2)]
    base = ar.off
    h_f = ar.alloc([NCH, TF], BF16)
    actb = [ar.alloc([2, TF], BF16) for _ in range(2)]
    rstd_f = [ar.alloc([TF], F32) for _ in range(2)]
    lnt_f = [ar.alloc([TF], F32) for _ in range(2)]
    sg = [ar.alloc([SUB], F32) for _ in range(2)]
    ffn_end = ar.off
    ar.off = base
    hoff = ar.off
    h_m = ar.alloc([NCH, TM], BF16)
    save = ar.off
    ar.off = hoff
    acc = ar.alloc([4, TM], F32)
    ar.off = save
    rstd_m = ar.alloc([TM], F32)
    lnt_m = ar.alloc([TM], F32)
    qT = ar.alloc([4, TM], BF16)
    kdT = ar.alloc([4, TM + P], BF16)
    Vd = ar.alloc([5, 256], BF16)
    zb = ar.alloc([4, TM + 30], BF16)
    NDG = 12
    dg = ar.alloc([NDG, P], BF16)
    pT = [ar.alloc([SUB], BF16) for _ in range(4)]
    rden = [ar.alloc([SUB], F32) for _ in range(2)]
    sgm = rden[0]
    mixT = ar.alloc([NCH, TM], BF16)
    cwt = ar.alloc([4, 34], F32)
    cwb = ar.alloc([4, 32], BF16)
    mean_t = lnt_m
    rstd2 = rstd_m
    cstage = rden[0]
    mix_end = ar.off
    ar.off = base
    stage = [ar.alloc([D], F32) for _ in range(2)]
    gstage = ar.alloc([D], F32)
    sink_sb = ar.alloc([DEPTH * NH], F32)
    yT = ar.alloc([NCH, SUB], F32)
    rstd_o = ar.alloc([SUB], F32)
    lnt_o = ar.alloc([SUB], F32)
    bias_st = ar.alloc([4, 512], F32)

    psum = nc.alloc_psum_tensor("psum", [P, 8, 512], F32)
    banks = [Buf() for _ in range(8)]
    bank_i = [0]

    bank_set = [list(range(8))]

    bank_ctr = {}

    def newbank(bs=None):
        if bs is None:
            bs = bank_set[0]
            i = bs[bank_i[0] % len(bs)]
            bank_i[0] += 1
        else:
            k = tuple(bs)
            n = bank_ctr.get(k, 0)
            bank_ctr[k] = n + 1
            i = bs[n % len(bs)]
        return psum[:, i, :], banks[i]

    B_x = [[Buf() for _ in range(S // SUB)] for _ in range(NCH)]
    B_ident = Buf(const=True); B_idb = Buf(const=True); B_ones = Buf(const=True); B_eps = Buf(const=True)
    B_gT = Buf(const=True); B_esink = Buf(const=True); B_bias = Buf(const=True)
    B_wi = [Buf(), Buf()]
    B_wo = [Buf(), Buf()]
    B_sq = [Buf(), Buf()]
    B_h = Buf()
    B_hs = [B_h, Buf()]
    B_rstd_f = [Buf(), Buf()]; B_lnt_f = [Buf(), Buf()]
    B_act = [Buf(), Buf()]
    B_rstd = Buf()
    B_lnt = Buf()
    B_sg = [Buf(), Buf()]
    B_q = Buf(); B_kd = Buf(); B_vd = Buf(); B_z = [Buf() for _ in range(4)]; B_acc = [Buf() for _ in range(4)]
    B_pT = [Buf() for _ in range(4)]; B_rden = [Buf(), Buf()]
    B_mixa = [Buf(), Buf()]
    B_mixc = [Buf() for _ in range(4)]
    B_cw = Buf()
    B_dummy = Buf()

    def act_prefetch(func):
        op("act", lambda e, func=func: e.activation(out=dummy_t[:], in_=eps_t[:], func=func), reads=[B_eps], writes=[B_dummy])
    B_dg = [Buf() for _ in range(16)]
    dg_ctr = [0]
    B_stage = [Buf(), Buf()]; B_gst = Buf(); B_yT = Buf()

    for n in ("d_wi0", "d_wi1", "d_wo0", "d_wo1", "d_st0", "d_st1", "d_misc", "d_out0", "d_out1"):
        pg.new_sem(n)
    wi_ctr = [0]
    wo_ctr = [0]
    sq_ctr = [0]

    def load_wi(parts):
        s = wi_ctr[0] % 2
        wi_ctr[0] += 1
        fns = []
        for (c0, ncol, src) in parts:
            fns.append(lambda e, c0=c0, ncol=ncol, src=src, s=s: e.dma_start(
                out=wi[s][:, :, c0:c0 + ncol], in_=src.rearrange("(kc p) n -> p kc n", p=P)))
        op("pool", fns, writes=[B_wi[s]], dma_sem="d_wi%d" % s, nofence=True)
        return s

    def load_wo(src):
        s = wo_ctr[0] % 2
        wo_ctr[0] += 1
        op("pool", [lambda e, s=s, src=src: e.dma_start(out=wo[s][:], in_=src.rearrange("(jc p) n -> p jc n", p=P))],
           writes=[B_wo[s]], dma_sem="d_wo%d" % s, nofence=True)
        return s

    def norm_stats(t0, T, rstd, lnt, Br, Bl):
        for s in range(T // SUB):
            ps, pb = newbank()
            xs = (t0 + s * SUB) // SUB
            lo, hi = t0 + s * SUB, t0 + (s + 1) * SUB
            for c in range(NCH):
                k = sq_ctr[0] % 2
                sq_ctr[0] += 1
                if c % 2 == 0:
                    op("act", lambda e, c=c, k=k, lo=lo, hi=hi: e.activation(out=sq[k][:], in_=xT[:, c, lo:hi], func=AF.Square),
                       reads=[B_x[c][xs]], writes=[B_sq[k]])
                else:
                    op("dve", lambda e, c=c, k=k, lo=lo, hi=hi: e.tensor_tensor(out=sq[k][:], in0=xT[:, c, lo:hi], in1=xT[:, c, lo:hi], op=ALU.mult),
                       reads=[B_x[c][xs]], writes=[B_sq[k]])
                op("pe", lambda e, c=c, k=k, ps=ps: e.matmul(ps, ones_b[:], sq[k][:], start=(c == 0), stop=(c == NCH - 1)),
                   reads=[B_sq[k], B_ones], writes=[pb])
            op("act", lambda e, ps=ps, s=s: e.activation(out=lnt[:, s * SUB:(s + 1) * SUB], in_=ps, func=AF.Ln, scale=1.0 / D, bias=eps_t[:]),
               reads=[pb, B_eps], writes=[Bl])
            op("act", lambda e, s=s: e.activation(out=rstd[:, s * SUB:(s + 1) * SUB], in_=lnt[:, s * SUB:(s + 1) * SUB], func=AF.Exp, scale=-0.5),
               reads=[Bl], writes=[Br])

    def norm_apply(t0, T, gidx, dst, rstd, Br, B_dsts):
        for s in range(T // SUB):
            xs = (t0 + s * SUB) // SUB
            lo, hi = t0 + s * SUB, t0 + (s + 1) * SUB
            for c in range(NCH):
                op("dve", lambda e, c=c, s=s, lo=lo, hi=hi: e.scalar_tensor_tensor(
                    out=dst[:, c, s * SUB:(s + 1) * SUB], in0=xT[:, c, lo:hi],
                    scalar=gT[:, c, gidx:gidx + 1], in1=rstd[:, s * SUB:(s + 1) * SUB], op0=ALU.mult, op1=ALU.mult),
                   reads=[B_x[c][xs], Br, B_gT], writes=[B_dsts[s]])

    def rmsnorm(t0, T, gidx, h, rstd, lnt, out_f32=None):
        dst = h if out_f32 is None else out_f32
        B_dst = B_h if out_f32 is None else B_yT
        norm_stats(t0, T, rstd, lnt, B_rstd, B_lnt)
        norm_apply(t0, T, gidx, dst, rstd, B_rstd, [B_dst] * (T // SUB))

    def resid_round(t0, nsub, slot_o, src_fn, src_bufs, scale, part=None, nparts=1):
        groups = [(o, s) for o in range(NCH) for s in range(nsub)]
        if part is not None:
            per = len(groups) // nparts
            groups = groups[part * per:(part + 1) * per]
        for (o, s) in groups:
            if True:
                ps, pb = newbank()
                op("pe", [lambda e, jj=jj, o=o, s=s, ps=ps: e.matmul(ps, wo[slot_o][:, jj, o * P:(o + 1) * P], src_fn(jj, s),
                                                                    start=(jj == 0), stop=(jj == 1)) for jj in range(2)],
                   reads=[B_wo[slot_o]] + src_bufs, writes=[pb])
                xs = (t0 + s * SUB) // SUB
                xv = xT[:, o, t0 + s * SUB: t0 + (s + 1) * SUB]
                op("dve", lambda e, ps=ps, xv=xv: e.scalar_tensor_tensor(out=xv, in0=ps, scalar=scale, in1=xv, op0=ALU.mult, op1=ALU.add),
                   reads=[pb, B_x[o][xs]], writes=[B_x[o][xs]])

    def resid_multi(t0, pieces, scale):
        n = 2 * len(pieces)
        xs = t0 // SUB
        for o in range(NCH):
            ps, pb = newbank()
            fns = []
            rd = []
            i = 0
            for (slot_o, src_fn, src_bufs) in pieces:
                rd += [B_wo[slot_o]] + src_bufs
                for jj in range(2):
                    fns.append(lambda e, jj=jj, o=o, ps=ps, slot_o=slot_o, src_fn=src_fn, i=i: e.matmul(ps, wo[slot_o][:, jj, o * P:(o + 1) * P], src_fn(jj), start=(i == 0), stop=(i == n - 1)))
                    i += 1
            op("pe", fns, reads=rd, writes=[pb])
            xv = xT[:, o, t0:t0 + SUB]
            op("dve", lambda e, ps=ps, xv=xv: e.scalar_tensor_tensor(out=xv, in0=ps, scalar=scale, in1=xv, op0=ALU.mult, op1=ALU.add),
               reads=[pb, B_x[o][xs]], writes=[B_x[o][xs]])

    def ffn_phase(l, w_in_d, w_out_d, gidx):
        norm_stats(0, TF, rstd_f[0], lnt_f[0], B_rstd_f[0], B_lnt_f[0])
        for t in range(NTF):
            t0 = t * TF
            nb = t % 2
            norm_apply(t0, TF, gidx, h_f, rstd_f[nb], B_rstd_f[nb], B_hs[:NSUB])
            pending = None
            for r in range(NR):
                if r == 3 and t + 1 < NTF:
                    norm_stats(t0 + TF, TF, rstd_f[1 - nb], lnt_f[1 - nb], B_rstd_f[1 - nb], B_lnt_f[1 - nb])
                si = load_wi([(0, 256, w_in_d[l, :, 256 * r:256 * r + 256]),
                              (256, 256, w_in_d[l, :, DFF + 256 * r: DFF + 256 * r + 256])])
                so = load_wo(w_out_d[l, 256 * r:256 * r + 256, :])
                ab = r % 2
                for jj in range(2):
                    for s in range(NSUB):
                        psg, pbg = newbank()
                        psu, pbu = newbank()
                        op("pe", [lambda e, kc=kc, jj=jj, psg=psg, s=s, si=si: e.matmul(psg, wi[si][:, kc, jj * P:(jj + 1) * P], h_f[:, kc, s * SUB:(s + 1) * SUB], start=(kc == 0), stop=(kc == NCH - 1)) for kc in range(NCH)],
                           reads=[B_wi[si], B_hs[s]], writes=[pbg])
                        op("pe", [lambda e, kc=kc, jj=jj, psu=psu, s=s, si=si: e.matmul(psu, wi[si][:, kc, 256 + jj * P:256 + (jj + 1) * P], h_f[:, kc, s * SUB:(s + 1) * SUB], start=(kc == 0), stop=(kc == NCH - 1)) for kc in range(NCH)],
                           reads=[B_wi[si], B_hs[s]], writes=[pbu])
                        k = (jj * NSUB + s) % 2
                        op("act", lambda e, psg=psg, k=k: e.activation(out=sg[k][:], in_=psg, func=AF.Silu), reads=[pbg], writes=[B_sg[k]])
                        op("dve", lambda e, psu=psu, k=k, jj=jj, s=s, ab=ab: e.tensor_tensor(out=actb[ab][:, jj, s * SUB:(s + 1) * SUB], in0=psu, in1=sg[k][:], op=ALU.mult),
                           reads=[pbu, B_sg[k]], writes=[B_act[ab]])
                        if pending is not None:
                            pending(jj * NSUB + s, 2 * NSUB)
                pending = (lambda part=None, nparts=1, so=so, ab=ab: resid_round(t0, NSUB, so, lambda jj, s, ab=ab: actb[ab][:, jj, s * SUB:(s + 1) * SUB], [B_act[ab]], 0.5, part, nparts))
            pending()

    def mixer_phase(l):
        op("sp", [lambda e: e.dma_start(out=cstage[0:CW, :], in_=wdw_d[l]),
                  lambda e: e.dma_start(out=cstage[CW:CW + 1, :], in_=bdw_d[l:l + 1, :]),
                  lambda e: e.dma_start(out=cstage[CW + 1:CW + 2, :], in_=lng_d[l:l + 1, :]),
                  lambda e: e.dma_start(out=cstage[CW + 2:CW + 3, :], in_=lnb_d[l:l + 1, :])],
           writes=[B_rden[0]], dma_sem="d_misc")
        for c in range(4):
            ps, pb = newbank()
            op("pe", lambda e, c=c, ps=ps: e.transpose(ps[:, 0:34], cstage[0:34, c * P:(c + 1) * P], ident_f[0:34, 0:34]),
               reads=[B_rden[0], B_ident], writes=[pb])
            op("dve", lambda e, c=c, ps=ps: e.tensor_copy(out=cwt[:, c, :], in_=ps[:, 0:34]), reads=[pb], writes=[B_cw])
            op("dve", lambda e, c=c, ps=ps: e.tensor_copy(out=cwb[:, c, 0:CW], in_=ps[:, 0:CW]), reads=[pb, B_cw], writes=[B_cw])
        gidx = DEPTH + l
        W = win_d[l]
        op("dve", lambda e: e.memset(kdT[:], 0.0), writes=[B_kd])
        pt_ctr = [0]
        for t in range(NTM):
            t0 = t * TM
            if t == 0:
                op("dve", lambda e: e.memset(zb[:, :, 0:30], 0.0), writes=B_z)
            else:
                op("dve", lambda e: e.tensor_copy(out=zb[:, :, 0:30], in_=zb[:, :, TM:TM + 30]), reads=B_z, writes=B_z)
                op("dve", lambda e: e.tensor_copy(out=kdT[:, :, 0:P], in_=kdT[:, :, TM:TM + P]), reads=[B_kd], writes=[B_kd])
                op("dve", lambda e: e.tensor_copy(out=Vd[:, 0, :], in_=Vd[:, 4, :]), reads=[B_vd], writes=[B_vd])
            rmsnorm(t0, TM, gidx, h_m, rstd_m, lnt_m)
            act_prefetch(AF.Sigmoid)
            si = load_wi([(0, 512, W[:, 0:512])])
            for c in range(4):
                ps, pb = newbank()
                op("pe", [lambda e, kc=kc, c=c, ps=ps, si=si: e.matmul(ps, wi[si][:, kc, c * P:(c + 1) * P], h_m[:, kc, :], start=(kc == 0), stop=(kc == NCH - 1)) for kc in range(NCH)],
                   reads=[B_wi[si], B_h], writes=[pb])
                op("act", lambda e, c=c, ps=ps: e.copy(out=qT[:, c, :], in_=ps), reads=[pb], writes=[B_q])
            si = load_wi([(0, 256, W[:, 512:768]), (256, HD, W[:, 576:640]), (320, HD, W[:, 512:576])])
            for v in range(2):
                ps, pb = newbank()
                op("pe", [lambda e, kc=kc, v=v, ps=ps, si=si: e.matmul(ps, wi[si][:, kc, v * 256:v * 256 + P], h_m[:, kc, :], start=(kc == 0), stop=(kc == NCH - 1)) for kc in range(NCH)],
                   reads=[B_wi[si], B_h], writes=[pb])
                kv_top, kv_bot = (0, 1) if v == 0 else (1, 0)
                op("act", lambda e, kv=kv_top, ps=ps: e.copy(out=kdT[0:HD, 2 * kv, P:P + TM], in_=ps[0:HD, :]), reads=[pb], writes=[B_kd])
                op("act", lambda e, kv=kv_bot, ps=ps: e.copy(out=kdT[HD:P, 2 * kv + 1, P:P + TM], in_=ps[HD:P, :]), reads=[pb, B_kd], writes=[B_kd])
            for b in range(4):
                ps, pb = newbank()
                op("pe", [lambda e, kc=kc, b=b, ps=ps, si=si: e.matmul(ps[:, 0:256], h_m[:, kc, b * P:(b + 1) * P],
                                                                      wi[si][:, kc, 128:256].rearrange("p (k d) -> p k d", k=2).unsqueeze(2).broadcast_to([P, 2, 2, HD]),
                                                                      start=(kc == 0), stop=(kc == NCH - 1)) for kc in range(NCH)],
                   reads=[B_wi[si], B_h], writes=[pb])
                op("act", lambda e, b=b, ps=ps: e.copy(out=Vd[:, b + 1, :], in_=ps[:, 0:256]), reads=[pb], writes=[B_vd])
            sa = load_wi([(0, 512, W[:, 768:1280])])
            sgt = load_wi([(0, 512, W[:, 1280:1792])])
            so_att = [load_wo(wout_d[l, 256 * r:256 * r + 256, :]) for r in range(2)]
            pa = []
            for c in range(4):
                psa, pba = newbank()
                op("pe", [lambda e, kc=kc, c=c, psa=psa, sa=sa: e.matmul(psa, wi[sa][:, kc, c * P:(c + 1) * P], h_m[:, kc, :], start=(kc == 0), stop=(kc == NCH - 1)) for kc in range(NCH)],
                   reads=[B_wi[sa], B_h], writes=[pba])
                pa.append((psa, pba))
            for c in range(4):
                psa, pba = pa[c]
                psg, pbg = newbank()
                op("pe", [lambda e, kc=kc, c=c, psg=psg, sgt=sgt: e.matmul(psg, wi[sgt][:, kc, c * P:(c + 1) * P], h_m[:, kc, :], start=(kc == 0), stop=(kc == NCH - 1)) for kc in range(NCH)],
                   reads=[B_wi[sgt], B_h], writes=[pbg])
                op("act", lambda e, psg=psg: e.activation(out=sgm[:], in_=psg, func=AF.Sigmoid), reads=[pbg], writes=[B_rden[0]])
                op("dve", lambda e, psa=psa, c=c: e.tensor_tensor(out=zb[:, c, 30:30 + TM], in0=psa, in1=sgm[:], op=ALU.mult),
                   reads=[pba, B_rden[0]], writes=[B_z[c]])
            act_prefetch(AF.Ln)

            bank_set[0] = list(range(6))
            ps1, pb1 = psum[:, 6, :], banks[6]
            ps2, pb2 = psum[:, 7, :], banks[7]

            ATT_B = [0, 1, 2, 3]
            CONV_B = [4, 5]
            conv_state = {}

            def conv_taps(c, ja, jb):
                if ja == 0:
                    conv_state[c] = newbank(CONV_B)
                psc, pbc = conv_state[c]
                j0 = ja
                while j0 < jb:
                    n = min(4, jb - j0)
                    sl = dg_ctr[0] % 3
                    dg_ctr[0] += 1
                    op("dve", lambda e, c=c, j0=j0, n=n, sl=sl: e.tensor_tensor(
                        out=dg[:, sl * 4:sl * 4 + n, :], in0=ident_b[:].unsqueeze(1).broadcast_to([P, n, P]),
                        in1=cwb[:, c, j0:j0 + n].unsqueeze(2).broadcast_to([P, n, P]), op=ALU.mult),
                       reads=[B_cw, B_idb], writes=[B_dg[sl]])
                    op("pe", [lambda e, c=c, j=j, sl=sl, j0=j0, psc=psc: e.matmul(psc, dg[:, sl * 4 + j - j0, :], zb[:, c, j:j + TM], start=(j == 0), stop=(j == CW - 1)) for j in range(j0, j0 + n)],
                       reads=[B_dg[sl], B_z[c]], writes=[pbc])
                    j0 += n

            def conv_evac(c):
                psc, pbc = conv_state[c]
                if True:
                    op("act", lambda e, c=c, psc=psc: e.activation(out=acc[:, c, :], in_=psc, func=AF.Identity, bias=cwt[:, c, 31:32]),
                       reads=[pbc, B_cw], writes=[B_acc[c], B_h])
                    op("act", lambda e, c=c, psc=psc: e.activation(out=sq[0][:], in_=psc, func=AF.Identity, bias=cwt[:, c, 31:32]), reads=[pbc, B_cw], writes=[B_sq[0]])
                    op("act", lambda e, c=c, psc=psc: e.activation(out=sq[1][:], in_=psc, func=AF.Square, bias=cwt[:, c, 31:32]), reads=[pbc, B_cw], writes=[B_sq[1]])

            def stats_mm(c):
                op("pe", [lambda e, c=c: e.matmul(ps1, ones_b[:], sq[0][:], start=(c == 0), stop=(c == 3)),
                          lambda e, c=c: e.matmul(ps2, ones_b[:], sq[1][:], start=(c == 0), stop=(c == 3))],
                   reads=[B_sq[0], B_sq[1], B_ones], writes=[pb1, pb2])

            def attn_scores(b, kv):
                nblk = t * 4 + b
                kbs = [1] if nblk == 0 else [0, 1]
                pts = []
                for kb in kbs:
                    slot = b + kb
                    ps, pb = newbank(ATT_B)
                    op("pe", [lambda e, kb=kb, kv=kv, ps=ps: e.matmul(ps, ident_b[:], bias_sb[:, kb * 2 + kv, :], start=True, stop=False),
                              lambda e, kv=kv, slot=slot, ps=ps, b=b: e.matmul(ps[:, 0:256], kdT[:, 2 * kv, slot * P:(slot + 1) * P], qT[:, 2 * kv:2 * kv + 2, b * P:(b + 1) * P], start=False, stop=False),
                              lambda e, kv=kv, slot=slot, ps=ps, b=b: e.matmul(ps[:, 256:512], kdT[:, 2 * kv + 1, slot * P:(slot + 1) * P], qT[:, 2 * kv:2 * kv + 2, b * P:(b + 1) * P], start=False, stop=True)],
                       reads=[B_idb, B_bias, B_kd, B_q], writes=[pb])
                    pi = pt_ctr[0] % 4
                    pt_ctr[0] += 1
                    op("act", lambda e, ps=ps, pi=pi: e.activation(out=pT[pi][:], in_=ps, func=AF.Exp, scale=0.125), reads=[pb], writes=[B_pT[pi]])
                    pts.append((pi, slot))
                return pts

            def attn_pv(b, kv, pts):
                psd, pbd = newbank(ATT_B)
                pso, pbo = newbank(ATT_B)
                fns = []
                npt = len(pts)
                for i, (pi, slot) in enumerate(pts):
                    fns.append(lambda e, pi=pi, i=i, psd=psd, npt=npt: e.matmul(psd, ones_b[:], pT[pi][:], start=(i == 0), stop=(i == npt - 1)))
                    fns.append(lambda e, pi=pi, i=i, slot=slot, kv=kv, pso=pso, npt=npt: e.matmul(pso, Vd[:, slot, kv * P:(kv + 1) * P], pT[pi][:], start=(i == 0), stop=(i == npt - 1)))
                op("pe", fns, reads=[B_ones, B_vd] + [B_pT[pi] for pi, _ in pts], writes=[pbd, pbo])
                return (psd, pbd, pso, pbo)

            def attn_norm(b, kv, outs):
                psd, pbd, pso, pbo = outs
                rd = rden[kv]
                for i in range(4):
                    hh = l * NH + 4 * kv + GPERM[i]
                    op("dve", lambda e, i=i, hh=hh, psd=psd, rd=rd: e.tensor_scalar(out=rd[:, i * P:(i + 1) * P], in0=psd[:, i * P:(i + 1) * P], scalar1=esink[:, hh:hh + 1], scalar2=None, op0=ALU.add),
                       reads=[pbd, B_esink], writes=[B_rden[kv]])
                op("act", lambda e, rd=rd: e.activation(out=rd[:], in_=rd[:], func=AF.Ln), reads=[B_rden[kv]], writes=[B_rden[kv]])
                op("act", lambda e, rd=rd: e.activation(out=rd[:], in_=rd[:], func=AF.Exp, scale=-1.0), reads=[B_rden[kv]], writes=[B_rden[kv]])
                for half in range(2):
                    pr = slice(half * HD, (half + 1) * HD)
                    cs = slice(half * 256, (half + 1) * 256)
                    op("dve", lambda e, pr=pr, cs=cs, kv=kv, b=b, pso=pso, rd=rd: e.tensor_tensor(
                        out=mixT[pr, 2 * kv:2 * kv + 2, b * P:(b + 1) * P], in0=pso[pr, cs].rearrange("p (a q) -> p a q", a=2),
                        in1=rd[pr, cs].rearrange("p (a q) -> p a q", a=2), op=ALU.mult),
                       reads=[pbo, B_rden[kv]], writes=[B_mixa[kv]])

            for c in range(4):
                conv_taps(c, 0, CW)
                if c > 0:
                    stats_mm(c - 1)
                conv_evac(c)
            stats_mm(3)
            op("dve", lambda e, ps1=ps1: e.tensor_scalar(out=mean_t[:], in0=ps1, scalar1=1.0 / 512, scalar2=None, op0=ALU.mult), reads=[pb1], writes=[B_lnt])
            op("dve", lambda e: e.tensor_tensor(out=rstd2[:], in0=mean_t[:], in1=mean_t[:], op=ALU.mult), reads=[B_lnt], writes=[B_rstd])
            op("dve", lambda e, ps2=ps2: e.scalar_tensor_tensor(out=rstd2[:], in0=ps2, scalar=1.0 / 512, in1=rstd2[:], op0=ALU.mult, op1=ALU.subtract), reads=[pb2, B_rstd], writes=[B_rstd])
            ATT_B[:] = list(range(8))
            units = [(u // 2, u % 2) for u in range(8)]
            pts_next = attn_scores(*units[0])
            for u in range(8):
                bb, kv = units[u]
                pts = pts_next
                if u + 1 < 8:
                    pts_next = attn_scores(*units[u + 1])
                outs = attn_pv(bb, kv, pts)
                attn_norm(bb, kv, outs)
            bank_set[0] = list(range(8))
            resid_multi(t0, [(so_att[r], (lambda jj, r=r: mixT[:, 2 * r + jj, :]), [B_mixa[r]]) for r in range(2)], 1.0)
            so_cv = [load_wo(wout_d[l, 256 * r:256 * r + 256, :]) for r in range(2, 4)]
            op("act", lambda e: e.activation(out=rstd2[:], in_=rstd2[:], func=AF.Ln, bias=eps_t[:]), reads=[B_rstd, B_eps], writes=[B_rstd])
            op("act", lambda e: e.activation(out=rstd2[:], in_=rstd2[:], func=AF.Exp, scale=-0.5), reads=[B_rstd], writes=[B_rstd])
            act_prefetch(AF.Silu)
            for c in range(4):
                op("dve", lambda e, c=c: e.tensor_tensor(out=acc[:, c, :], in0=acc[:, c, :], in1=mean_t[:], op=ALU.subtract), reads=[B_acc[c], B_lnt], writes=[B_acc[c]])
                op("dve", lambda e, c=c: e.tensor_tensor(out=acc[:, c, :], in0=acc[:, c, :], in1=rstd2[:], op=ALU.mult), reads=[B_acc[c], B_rstd], writes=[B_acc[c]])
                op("act", lambda e, c=c: e.activation(out=mixT[:, 4 + c, :], in_=acc[:, c, :], func=AF.Silu, scale=cwt[:, c, 32:33], bias=cwt[:, c, 33:34]),
                   reads=[B_acc[c], B_cw, B_h], writes=[B_mixc[c]])
            act_prefetch(AF.Ln)
            resid_multi(t0, [(so_cv[r - 2], (lambda jj, r=r: mixT[:, 2 * r + jj, :]), [B_mixc[2 * (r - 2)], B_mixc[2 * (r - 2) + 1]]) for r in range(2, 4)], 1.0)

    op("pool", lambda e: e.memset(ident_f[:], 0.0), writes=[B_ident])
    op("pool", lambda e: e.affine_select(out=ident_f[:], in_=ident_f[:], pattern=[[-1, P]], compare_op=ALU.not_equal, fill=1.0, base=0, channel_multiplier=1),
       reads=[B_ident], writes=[B_ident])
    op("pool", lambda e: e.tensor_copy(out=ident_b[:], in_=ident_f[:]), reads=[B_ident], writes=[B_idb])
    op("pool", lambda e: e.memset(ones_b[:], 1.0), writes=[B_ones])
    op("pool", lambda e: e.memset(eps_t[:], EPS), writes=[B_eps])
    NG = 3 * DEPTH + 1
    op("sp", [lambda e: e.dma_start(out=gstage[0:DEPTH, :], in_=nf1_d),
              lambda e: e.dma_start(out=gstage[DEPTH:2 * DEPTH, :], in_=nmx_d),
              lambda e: e.dma_start(out=gstage[2 * DEPTH:3 * DEPTH, :], in_=nf2_d),
              lambda e: e.dma_start(out=gstage[3 * DEPTH:NG, :], in_=fn_d.rearrange("(o n) -> o n", o=1)),
              lambda e: e.dma_start(out=sink_sb[:], in_=sinks_d.rearrange("a b -> (a b)").partition_broadcast(P)),
              lambda e: e.dma_start(out=bias_st[:], in_=bias8_d.rearrange("a b p n -> p (a b) n"))],
       writes=[B_gst], dma_sem="d_misc")
    for c in range(NCH):
        ps, pb = newbank()
        op("pe", lambda e, c=c, ps=ps: e.transpose(ps[:, 0:NG], gstage[0:NG, c * P:(c + 1) * P], ident_f[0:NG, 0:NG]), reads=[B_gst, B_ident], writes=[pb])
        op("dve", lambda e, c=c, ps=ps: e.tensor_copy(out=gT[:, c, 0:NG], in_=ps[:, 0:NG]), reads=[pb], writes=[B_gT])
    op("act", lambda e: e.activation(out=esink[:], in_=sink_sb[:], func=AF.Exp), reads=[B_gst], writes=[B_esink])
    op("dve", lambda e: e.tensor_copy(out=bias_sb[:], in_=bias_st[:]), reads=[B_gst], writes=[B_bias])
    for blk in range(NBLK):
        s = blk % 2
        op("sp", [lambda e, s=s, blk=blk: e.dma_start(out=stage[s][:], in_=x_d[blk * P:(blk + 1) * P, :])], writes=[B_stage[s]], dma_sem="d_st%d" % s)
        for half in range(2):
            ps, pb = newbank()
            op("pe", [lambda e, s=s, c=c, ps=ps, half=half: e.transpose(ps[:, c * P:(c + 1) * P], stage[s][:, (half * 4 + c) * P:(half * 4 + c + 1) * P], ident_f[:]) for c in range(4)],
               reads=[B_stage[s], B_ident], writes=[pb])
            xs = blk * P // SUB
            dst = xT[:, half * 4:half * 4 + 4, blk * P:(blk + 1) * P]
            src = ps.rearrange("p (c q) -> p c q", c=4)
            if half == 0:
                op("act", lambda e, dst=dst, src=src: e.copy(out=dst, in_=src), reads=[pb], writes=[B_x[half * 4 + c][xs] for c in range(4)])
            else:
                op("dve", lambda e, dst=dst, src=src: e.tensor_copy(out=dst, in_=src), reads=[pb], writes=[B_x[half * 4 + c][xs] for c in range(4)])

    import os
    stages = os.environ.get("KSTAGES", "f1,mix,f2").split(",")
    for l in range(DEPTH):
        if "f1" in stages:
            if l == 0 or "mix" not in stages or "f2" not in stages:
                pg.fence()
            ffn_phase(l, w1i_d, w1o_d, 0 * DEPTH + l)
        if "mix" in stages:
            pg.fence()
            mixer_phase(l)
        if "f2" in stages:
            pg.fence()
            ffn_phase(l, w2i_d, w2o_d, 2 * DEPTH + l)

    pg.fence()
    last_tok = [None, None]
    oi = 0
    for t in range(S // SUB):
        t0 = t * SUB
        rmsnorm(t0, SUB, 3 * DEPTH, None, rstd_o, lnt_o, out_f32=yT)
        for b in range(4):
            s = oi % 2
            oi += 1
            for half in range(2):
                ps, pb = newbank()
                op("pe", [lambda e, c=c, ps=ps, half=half, b=b: e.transpose(ps[:, c * P:(c + 1) * P], yT[:, half * 4 + c, b * P:(b + 1) * P], ident_f[:]) for c in range(4)],
                   reads=[B_yT, B_ident], writes=[pb])
                if half == 0:
                    op("act", lambda e, s=s, ps=ps: e.copy(out=stage[s][:, 0:512], in_=ps), reads=[pb], writes=[B_stage[s]])
                else:
                    op("dve", lambda e, s=s, ps=ps: e.tensor_copy(out=stage[s][:, 512:1024], in_=ps), reads=[pb, B_stage[s]], writes=[B_stage[s]])
            r0 = t0 + b * P
            last_tok[s] = op("sp", [lambda e, s=s, r0=r0: e.dma_start(out=out_d[r0:r0 + P, :], in_=stage[s][:])], reads=[B_stage[s]], dma_sem="d_out%d" % s)
    pg.final_wait("sp", [tk for tk in last_tok if tk is not None])
    pg.emit()
    es.close()
    return nc


def make_bias8():
    out = np.zeros((2, 2, P, 4, P), np.float32)
    s = np.arange(P)[:, None]
    q = np.arange(P)[None, :]
    for kb in range(2):
        dist = (q - s + P) if kb == 0 else (q - s)
        valid = (dist >= 0) & (dist < P)
        for kv in range(2):
            for i in range(4):
                hh = 4 * kv + GPERM[i]
                slope = 2.0 ** (-(hh + 1))
                out[kb, kv, :, i, :] = np.where(valid, -8.0 * slope * dist, -240000.0)
    return out.reshape(2, 2, P, 512)


_NAMES = ["norm_ffn1", "w_ffn1_in", "w_ffn1_out", "norm_mix", "w_in", "sinks", "w_dw", "b_dw",
          "conv_ln_g", "conv_ln_b", "w_out", "norm_ffn2", "w_ffn2_in", "w_ffn2_out", "final_norm"]


def kernel(**inputs):
    x = np.ascontiguousarray(inputs["x"], dtype=np.float32)
    B, S, _ = x.shape
    DEPTH = inputs["w_in"].shape[0]
    nc = build_program(DEPTH, S)
    shared = {n: np.ascontiguousarray(inputs[n], dtype=np.float32) for n in _NAMES}
    shared["bias8"] = make_bias8()
    in_maps = [dict(shared, x=x[b]) for b in range(B)]
    res = run_bass_kernel_spmd(nc, in_maps, core_ids=list(range(B)))
    return np.stack([np.asarray(r["out"], dtype=np.float32) for r in res.results], axis=0)
```

```python
import numpy as np
from contextlib import ExitStack
import concourse.bass as bass
import concourse.mybir as mybir
from concourse.bass_utils import run_bass_kernel_spmd

F32 = mybir.dt.float32
BF16 = mybir.dt.bfloat16
AF = mybir.ActivationFunctionType
ALU = mybir.AluOpType

D = 1024
NCH = 8
DFF = 2816
NR = 11
INW = 1792
NH = 8
HD = 64
CW = 31
EPS = 1e-6
P = 128
SUB = 512
GPERM = (0, 2, 1, 3)


class Buf:
    __slots__ = ("w", "r", "const")

    def __init__(self, const=False):
        self.w = None
        self.r = []
        self.const = const


class Prog:
    ENGS = ("pe", "act", "dve", "pool", "sp")

    def __init__(self, nc, es):
        self.nc = nc
        self.es = es
        self.streams = {e: [] for e in self.ENGS}
        self.sems = {}
        self.count = {}
        self.waited = {e: {} for e in self.ENGS}
        self.fence_deps = {}
        for e in ("pe", "act", "dve", "pool"):
            self.new_sem("c_" + e)

    def fence(self):
        self.fence_deps = {"c_" + e: self.count["c_" + e] for e in ("pe", "act", "dve", "pool") if self.count["c_" + e] > 0}

    def new_sem(self, name):
        self.sems[name] = self.es.enter_context(self.nc.semaphore(name))
        self.count[name] = 0
        return name

    def op(self, eng, fns, reads=(), writes=(), dma_sem=None, nofence=False):
        if not isinstance(fns, (list, tuple)):
            fns = [fns]
        deps = {} if nofence else dict(self.fence_deps)

        def add(tok):
            if tok is not None and deps.get(tok[0], 0) < tok[1]:
                deps[tok[0]] = tok[1]

        for b in reads:
            add(b.w)
        for b in writes:
            add(b.w)
            for r in b.r:
                add(r)
        st = self.streams[eng]
        wd = self.waited[eng]
        waits = []
        for sname, val in deps.items():
            if eng == "pe" and sname == "c_pe":
                continue
            if wd.get(sname, 0) >= val:
                continue
            wd[sname] = val
            waits.append((sname, val))
        if dma_sem is not None:
            sname = dma_sem
            self.count[sname] += 16 * len(fns)
            tok = (sname, self.count[sname])
            st.append(("d", fns, sname, waits))
        else:
            sname = "c_" + eng
            self.count[sname] += 1
            tok = (sname, self.count[sname])
            st.append(("o", fns, sname, waits))
        for b in reads:
            if not b.const:
                b.r.append(tok)
        for b in writes:
            b.w = tok
            b.r = []
        return tok

    def final_wait(self, eng, toks):
        for tok in toks:
            self.streams[eng].append(("w", tok[0], tok[1]))

    def emit(self):
        nc = self.nc
        block = self.es.enter_context(nc.Block())
        sems = self.sems

        def run(stream):
            def f(e):
                for it in stream:
                    if it[0] == "w":
                        e.wait_ge(sems[it[1]], it[2])
                        continue
                    fns, sname, waits = it[1], it[2], it[3]
                    for (sn, v) in waits[:-1]:
                        e.wait_ge(sems[sn], v)
                    inc = 16 if it[0] == "d" else 1
                    for k, fn in enumerate(fns):
                        ins = fn(e)
                        if k == 0 and waits:
                            ins = ins._wait_ge(sems[waits[-1][0]], waits[-1][1])
                        if it[0] == "d" or k == len(fns) - 1:
                            ins.then_inc(sems[sname], inc)
            return f

        block.tensor(run(self.streams["pe"]))
        block.scalar(run(self.streams["act"]))
        block.vector(run(self.streams["dve"]))
        block.gpsimd(run(self.streams["pool"]))
        block.sync(run(self.streams["sp"]))


class Arena:
    def __init__(self, nc, nbytes):
        self.t = nc.alloc_sbuf_tensor("arena", [P, nbytes // 2], BF16)
        self.off = 0
        self.hi = 0
        self.cap = nbytes

    def alloc(self, free_shape, dt):
        esz = 4 if dt == F32 else 2
        n = 1
        for s in free_shape:
            n *= s
        nb = (n * esz + 63) // 64 * 64
        a = self.off
        self.off += nb
        self.hi = max(self.hi, self.off)
        assert self.off <= self.cap, (self.off, self.cap)
        ap = self.t[:, a // 2:(a + n * esz) // 2]
        if dt == F32:
            ap = ap.bitcast(F32)
        if len(free_shape) == 2:
            ap = ap.rearrange("p (a b) -> p a b", a=free_shape[0])
        elif len(free_shape) == 3:
            ap = ap.rearrange("p (a b c) -> p a b c", a=free_shape[0], b=free_shape[1])
        return ap


def build_program(DEPTH, S, TF=1024):
    NBLK = S // P
    TM = SUB
    NTM = S // TM
    TF = min(TF, S)
    NTF = S // TF
    NSUB = TF // SUB
    nc = bass.Bass("TRN2", target_bir_lowering=False)

    def din(name, shape):
        return nc.dram_tensor(name, list(shape), F32, kind="ExternalInput").ap()

    x_d = din("x", [S, D])
    nf1_d = din("norm_ffn1", [DEPTH, D])
    w1i_d = din("w_ffn1_in", [DEPTH, D, 2 * DFF])
    w1o_d = din("w_ffn1_out", [DEPTH, DFF, D])
    nmx_d = din("norm_mix", [DEPTH, D])
    win_d = din("w_in", [DEPTH, D, INW])
    sinks_d = din("sinks", [DEPTH, NH])
    wdw_d = din("w_dw", [DEPTH, CW, 512])
    bdw_d = din("b_dw", [DEPTH, 512])
    lng_d = din("conv_ln_g", [DEPTH, 512])
    lnb_d = din("conv_ln_b", [DEPTH, 512])
    wout_d = din("w_out", [DEPTH, D, D])
    nf2_d = din("norm_ffn2", [DEPTH, D])
    w2i_d = din("w_ffn2_in", [DEPTH, D, 2 * DFF])
    w2o_d = din("w_ffn2_out", [DEPTH, DFF, D])
    fn_d = din("final_norm", [D])
    bias8_d = din("bias8", [2, 2, P, 512])
    out_d = nc.dram_tensor("out", [S, D], F32, kind="ExternalOutput").ap()

    es = ExitStack()
    pg = Prog(nc, es)
    op = pg.op

    ar = Arena(nc, 212736)
    xT = ar.alloc([NCH, S], F32)
    ident_f = ar.alloc([P], F32)
    ident_b = ar.alloc([P], BF16)
    ones_b = ar.alloc([P], BF16)
    gT = ar.alloc([NCH, 16], F32)
    bias_sb = ar.alloc([4, 512], BF16)
    esink = ar.alloc([DEPTH * NH], F32)
    eps_t = ar.alloc([1], F32)
    dummy_t = ar.alloc([1], F32)
    wi = [ar.alloc([NCH, 512], BF16) for _ in range(2)]
    wo = [ar.alloc([2, D], BF16) for _ in range(2)]
    sq = [ar.alloc([SUB], BF16) for _ in range(2)]
    base = ar.off
    h_f = ar.alloc([NCH, TF], BF16)
    actb = [ar.alloc([2, TF], BF16) for _ in range(2)]
    rstd_f = [ar.alloc([TF], F32) for _ in range(2)]
    lnt_f = [ar.alloc([TF], F32) for _ in range(2)]
    sg = [ar.alloc([SUB], F32) for _ in range(2)]
    ffn_end = ar.off
    ar.off = base
    hoff = ar.off
    h_m = ar.alloc([NCH, TM], BF16)
    save = ar.off
    ar.off = hoff
    acc = ar.alloc([4, TM], F32)
    ar.off = save
    rstd_m = ar.alloc([TM], F32)
    lnt_m = ar.alloc([TM], F32)
    qT = ar.alloc([4, TM], BF16)
    kdT = ar.alloc([4, TM + P], BF16)
    Vd = ar.alloc([5, 256], BF16)
    zb = ar.alloc([4, TM + 30], BF16)
    NDG = 12
    dg = ar.alloc([NDG, P], BF16)
    pT = [ar.alloc([SUB], BF16) for _ in range(4)]
    rden = [ar.alloc([SUB], F32) for _ in range(2)]
    sgm = rden[0]
    mixT = ar.alloc([NCH, TM], BF16)
    cwt = ar.alloc([4, 34], F32)
    cwb = ar.alloc([4, 32], BF16)
    mean_t = lnt_m
    rstd2 = rstd_m
    cstage = rden[0]
    mix_end = ar.off
    ar.off = base
    stage = [ar.alloc([D], F32) for _ in range(2)]
    gstage = ar.alloc([D], F32)
    sink_sb = ar.alloc([DEPTH * NH], F32)
    yT = ar.alloc([NCH, SUB], F32)
    rstd_o = ar.alloc([SUB], F32)
    lnt_o = ar.alloc([SUB], F32)
    bias_st = ar.alloc([4, 512], F32)

    psum = nc.alloc_psum_tensor("psum", [P, 8, 512], F32)
    banks = [Buf() for _ in range(8)]
    bank_i = [0]

    bank_set = [list(range(8))]

    bank_ctr = {}

    def newbank(bs=None):
        if bs is None:
            bs = bank_set[0]
            i = bs[bank_i[0] % len(bs)]
            bank_i[0] += 1
        else:
            k = tuple(bs)
            n = bank_ctr.get(k, 0)
            bank_ctr[k] = n + 1
            i = bs[n % len(bs)]
        return psum[:, i, :], banks[i]

    B_x = [[Buf() for _ in range(S // SUB)] for _ in range(NCH)]
    B_ident = Buf(const=True); B_idb = Buf(const=True); B_ones = Buf(const=True); B_eps = Buf(const=True)
    B_gT = Buf(const=True); B_esink = Buf(const=True); B_bias = Buf(const=True)
    B_wi = [Buf(), Buf()]
    B_wo = [Buf(), Buf()]
    B_sq = [Buf(), Buf()]
    B_h = Buf()
    B_hs = [B_h, Buf()]
    B_rstd_f = [Buf(), Buf()]; B_lnt_f = [Buf(), Buf()]
    B_act = [Buf(), Buf()]
    B_rstd = Buf()
    B_lnt = Buf()
    B_sg = [Buf(), Buf()]
    B_q = Buf(); B_kd = Buf(); B_vd = Buf(); B_z = [Buf() for _ in range(4)]; B_acc = [Buf() for _ in range(4)]
    B_pT = [Buf() for _ in range(4)]; B_rden = [Buf(), Buf()]
    B_mixa = [Buf(), Buf()]
    B_mixc = [Buf() for _ in range(4)]
    B_cw = Buf()
    B_dummy = Buf()

    def act_prefetch(func):
        op("act", lambda e, func=func: e.activation(out=dummy_t[:], in_=eps_t[:], func=func), reads=[B_eps], writes=[B_dummy])
    B_dg = [Buf() for _ in range(16)]
    dg_ctr = [0]
    B_stage = [Buf(), Buf()]; B_gst = Buf(); B_yT = Buf()

    for n in ("d_wi0", "d_wi1", "d_wo0", "d_wo1", "d_st0", "d_st1", "d_misc", "d_out0", "d_out1"):
        pg.new_sem(n)
    wi_ctr = [0]
    wo_ctr = [0]
    sq_ctr = [0]

    def load_wi(parts):
        s = wi_ctr[0] % 2
        wi_ctr[0] += 1
        fns = []
        for (c0, ncol, src) in parts:
            fns.append(lambda e, c0=c0, ncol=ncol, src=src, s=s: e.dma_start(
                out=wi[s][:, :, c0:c0 + ncol], in_=src.rearrange("(kc p) n -> p kc n", p=P)))
        op("pool", fns, writes=[B_wi[s]], dma_sem="d_wi%d" % s, nofence=True)
        return s

    def load_wo(src):
        s = wo_ctr[0] % 2
        wo_ctr[0] += 1
        op("pool", [lambda e, s=s, src=src: e.dma_start(out=wo[s][:], in_=src.rearrange("(jc p) n -> p jc n", p=P))],
           writes=[B_wo[s]], dma_sem="d_wo%d" % s, nofence=True)
        return s

    def norm_stats(t0, T, rstd, lnt, Br, Bl):
        for s in range(T // SUB):
            ps, pb = newbank()
            xs = (t0 + s * SUB) // SUB
            lo, hi = t0 + s * SUB, t0 + (s + 1) * SUB
            for c in range(NCH):
                k = sq_ctr[0] % 2
                sq_ctr[0] += 1
                if c % 2 == 0:
                    op("act", lambda e, c=c, k=k, lo=lo, hi=hi: e.activation(out=sq[k][:], in_=xT[:, c, lo:hi], func=AF.Square),
                       reads=[B_x[c][xs]], writes=[B_sq[k]])
                else:
                    op("dve", lambda e, c=c, k=k, lo=lo, hi=hi: e.tensor_tensor(out=sq[k][:], in0=xT[:, c, lo:hi], in1=xT[:, c, lo:hi], op=ALU.mult),
                       reads=[B_x[c][xs]], writes=[B_sq[k]])
                op("pe", lambda e, c=c, k=k, ps=ps: e.matmul(ps, ones_b[:], sq[k][:], start=(c == 0), stop=(c == NCH - 1)),
                   reads=[B_sq[k], B_ones], writes=[pb])
            op("act", lambda e, ps=ps, s=s: e.activation(out=lnt[:, s * SUB:(s + 1) * SUB], in_=ps, func=AF.Ln, scale=1.0 / D, bias=eps_t[:]),
               reads=[pb, B_eps], writes=[Bl])
            op("act", lambda e, s=s: e.activation(out=rstd[:, s * SUB:(s + 1) * SUB], in_=lnt[:, s * SUB:(s + 1) * SUB], func=AF.Exp, scale=-0.5),
               reads=[Bl], writes=[Br])

    def norm_apply(t0, T, gidx, dst, rstd, Br, B_dsts):
        for s in range(T // SUB):
            xs = (t0 + s * SUB) // SUB
            lo, hi = t0 + s * SUB, t0 + (s + 1) * SUB
            for c in range(NCH):
                op("dve", lambda e, c=c, s=s, lo=lo, hi=hi: e.scalar_tensor_tensor(
                    out=dst[:, c, s * SUB:(s + 1) * SUB], in0=xT[:, c, lo:hi],
                    scalar=gT[:, c, gidx:gidx + 1], in1=rstd[:, s * SUB:(s + 1) * SUB], op0=ALU.mult, op1=ALU.mult),
                   reads=[B_x[c][xs], Br, B_gT], writes=[B_dsts[s]])

    def rmsnorm(t0, T, gidx, h, rstd, lnt, out_f32=None):
        dst = h if out_f32 is None else out_f32
        B_dst = B_h if out_f32 is None else B_yT
        norm_stats(t0, T, rstd, lnt, B_rstd, B_lnt)
        norm_apply(t0, T, gidx, dst, rstd, B_rstd, [B_dst] * (T // SUB))

    def resid_round(t0, nsub, slot_o, src_fn, src_bufs, scale, part=None, nparts=1):
        groups = [(o, s) for o in range(NCH) for s in range(nsub)]
        if part is not None:
            per = len(groups) // nparts
            groups = groups[part * per:(part + 1) * per]
        for (o, s) in groups:
            if True:
                ps, pb = newbank()
                op("pe", [lambda e, jj=jj, o=o, s=s, ps=ps: e.matmul(ps, wo[slot_o][:, jj, o * P:(o + 1) * P], src_fn(jj, s),
                                                                    start=(jj == 0), stop=(jj == 1)) for jj in range(2)],
                   reads=[B_wo[slot_o]] + src_bufs, writes=[pb])
                xs = (t0 + s * SUB) // SUB
                xv = xT[:, o, t0 + s * SUB: t0 + (s + 1) * SUB]
                op("dve", lambda e, ps=ps, xv=xv: e.scalar_tensor_tensor(out=xv, in0=ps, scalar=scale, in1=xv, op0=ALU.mult, op1=ALU.add),
                   reads=[pb, B_x[o][xs]], writes=[B_x[o][xs]])

    def resid_multi(t0, pieces, scale):
        n = 2 * len(pieces)
        xs = t0 // SUB
        for o in range(NCH):
            ps, pb = newbank()
            fns = []
            rd = []
            i = 0
            for (slot_o, src_fn, src_bufs) in pieces:
                rd += [B_wo[slot_o]] + src_bufs
                for jj in range(2):
                    fns.append(lambda e, jj=jj, o=o, ps=ps, slot_o=slot_o, src_fn=src_fn, i=i: e.matmul(ps, wo[slot_o][:, jj, o * P:(o + 1) * P], src_fn(jj), start=(i == 0), stop=(i == n - 1)))
                    i += 1
            op("pe", fns, reads=rd, writes=[pb])
            xv = xT[:, o, t0:t0 + SUB]
            op("dve", lambda e, ps=ps, xv=xv: e.scalar_tensor_tensor(out=xv, in0=ps, scalar=scale, in1=xv, op0=ALU.mult, op1=ALU.add),
               reads=[pb, B_x[o][xs]], writes=[B_x[o][xs]])

    def ffn_phase(l, w_in_d, w_out_d, gidx):
        norm_stats(0, TF, rstd_f[0], lnt_f[0], B_rstd_f[0], B_lnt_f[0])
        for t in range(NTF):
            t0 = t * TF
            nb = t % 2
            norm_apply(t0, TF, gidx, h_f, rstd_f[nb], B_rstd_f[nb], B_hs[:NSUB])
            pending = None
            for r in range(NR):
                if r == 3 and t + 1 < NTF:
                    norm_stats(t0 + TF, TF, rstd_f[1 - nb], lnt_f[1 - nb], B_rstd_f[1 - nb], B_lnt_f[1 - nb])
                si = load_wi([(0, 256, w_in_d[l, :, 256 * r:256 * r + 256]),
                              (256, 256, w_in_d[l, :, DFF + 256 * r: DFF + 256 * r + 256])])
                so = load_wo(w_out_d[l, 256 * r:256 * r + 256, :])
                ab = r % 2
                for jj in range(2):
                    for s in range(NSUB):
                        psg, pbg = newbank()
                        psu, pbu = newbank()
                        op("pe", [lambda e, kc=kc, jj=jj, psg=psg, s=s, si=si: e.matmul(psg, wi[si][:, kc, jj * P:(jj + 1) * P], h_f[:, kc, s * SUB:(s + 1) * SUB], start=(kc == 0), stop=(kc == NCH - 1)) for kc in range(NCH)],
                           reads=[B_wi[si], B_hs[s]], writes=[pbg])
                        op("pe", [lambda e, kc=kc, jj=jj, psu=psu, s=s, si=si: e.matmul(psu, wi[si][:, kc, 256 + jj * P:256 + (jj + 1) * P], h_f[:, kc, s * SUB:(s + 1) * SUB], start=(kc == 0), stop=(kc == NCH - 1)) for kc in range(NCH)],
                           reads=[B_wi[si], B_hs[s]], writes=[pbu])
                        k = (jj * NSUB + s) % 2
                        op("act", lambda e, psg=psg, k=k: e.activation(out=sg[k][:], in_=psg, func=AF.Silu), reads=[pbg], writes=[B_sg[k]])
                        op("dve", lambda e, psu=psu, k=k, jj=jj, s=s, ab=ab: e.tensor_tensor(out=actb[ab][:, jj, s * SUB:(s + 1) * SUB], in0=psu, in1=sg[k][:], op=ALU.mult),
                           reads=[pbu, B_sg[k]], writes=[B_act[ab]])
                        if pending is not None:
                            pending(jj * NSUB + s, 2 * NSUB)
                pending = (lambda part=None, nparts=1, so=so, ab=ab: resid_round(t0, NSUB, so, lambda jj, s, ab=ab: actb[ab][:, jj, s * SUB:(s + 1) * SUB], [B_act[ab]], 0.5, part, nparts))
            pending()

    def mixer_phase(l):
        op("sp", [lambda e: e.dma_start(out=cstage[0:CW, :], in_=wdw_d[l]),
                  lambda e: e.dma_start(out=cstage[CW:CW + 1, :], in_=bdw_d[l:l + 1, :]),
                  lambda e: e.dma_start(out=cstage[CW + 1:CW + 2, :], in_=lng_d[l:l + 1, :]),
                  lambda e: e.dma_start(out=cstage[CW + 2:CW + 3, :], in_=lnb_d[l:l + 1, :])],
           writes=[B_rden[0]], dma_sem="d_misc")
        for c in range(4):
            ps, pb = newbank()
            op("pe", lambda e, c=c, ps=ps: e.transpose(ps[:, 0:34], cstage[0:34, c * P:(c + 1) * P], ident_f[0:34, 0:34]),
               reads=[B_rden[0], B_ident], writes=[pb])
            op("dve", lambda e, c=c, ps=ps: e.tensor_copy(out=cwt[:, c, :], in_=ps[:, 0:34]), reads=[pb], writes=[B_cw])
            op("dve", lambda e, c=c, ps=ps: e.tensor_copy(out=cwb[:, c, 0:CW], in_=ps[:, 0:CW]), reads=[pb, B_cw], writes=[B_cw])
        gidx = DEPTH + l
        W = win_d[l]
        op("dve", lambda e: e.memset(kdT[:], 0.0), writes=[B_kd])
        pt_ctr = [0]
        for t in range(NTM):
            t0 = t * TM
            if t == 0:
                op("dve", lambda e: e.memset(zb[:, :, 0:30], 0.0), writes=B_z)
            else:
                op("dve", lambda e: e.tensor_copy(out=zb[:, :, 0:30], in_=zb[:, :, TM:TM + 30]), reads=B_z, writes=B_z)
                op("dve", lambda e: e.tensor_copy(out=kdT[:, :, 0:P], in_=kdT[:, :, TM:TM + P]), reads=[B_kd], writes=[B_kd])
                op("dve", lambda e: e.tensor_copy(out=Vd[:, 0, :], in_=Vd[:, 4, :]), reads=[B_vd], writes=[B_vd])
            rmsnorm(t0, TM, gidx, h_m, rstd_m, lnt_m)
            act_prefetch(AF.Sigmoid)
            si = load_wi([(0, 512, W[:, 0:512])])
            for c in range(4):
                ps, pb = newbank()
                op("pe", [lambda e, kc=kc, c=c, ps=ps, si=si: e.matmul(ps, wi[si][:, kc, c * P:(c + 1) * P], h_m[:, kc, :], start=(kc == 0), stop=(kc == NCH - 1)) for kc in range(NCH)],
                   reads=[B_wi[si], B_h], writes=[pb])
                op("act", lambda e, c=c, ps=ps: e.copy(out=qT[:, c, :], in_=ps), reads=[pb], writes=[B_q])
            si = load_wi([(0, 256, W[:, 512:768]), (256, HD, W[:, 576:640]), (320, HD, W[:, 512:576])])
            for v in range(2):
                ps, pb = newbank()
                op("pe", [lambda e, kc=kc, v=v, ps=ps, si=si: e.matmul(ps, wi[si][:, kc, v * 256:v * 256 + P], h_m[:, kc, :], start=(kc == 0), stop=(kc == NCH - 1)) for kc in range(NCH)],
                   reads=[B_wi[si], B_h], writes=[pb])
                kv_top, kv_bot = (0, 1) if v == 0 else (1, 0)
                op("act", lambda e, kv=kv_top, ps=ps: e.copy(out=kdT[0:HD, 2 * kv, P:P + TM], in_=ps[0:HD, :]), reads=[pb], writes=[B_kd])
                op("act", lambda e, kv=kv_bot, ps=ps: e.copy(out=kdT[HD:P, 2 * kv + 1, P:P + TM], in_=ps[HD:P, :]), reads=[pb, B_kd], writes=[B_kd])
            for b in range(4):
                ps, pb = newbank()
                op("pe", [lambda e, kc=kc, b=b, ps=ps, si=si: e.matmul(ps[:, 0:256], h_m[:, kc, b * P:(b + 1) * P],
                                                                      wi[si][:, kc, 128:256].rearrange("p (k d) -> p k d", k=2).unsqueeze(2).broadcast_to([P, 2, 2, HD]),
                                                                      start=(kc == 0), stop=(kc == NCH - 1)) for kc in range(NCH)],
                   reads=[B_wi[si], B_h], writes=[pb])
                op("act", lambda e, b=b, ps=ps: e.copy(out=Vd[:, b + 1, :], in_=ps[:, 0:256]), reads=[pb], writes=[B_vd])
            sa = load_wi([(0, 512, W[:, 768:1280])])
            sgt = load_wi([(0, 512, W[:, 1280:1792])])
            so_att = [load_wo(wout_d[l, 256 * r:256 * r + 256, :]) for r in range(2)]
            pa = []
            for c in range(4):
                psa, pba = newbank()
                op("pe", [lambda e, kc=kc, c=c, psa=psa, sa=sa: e.matmul(psa, wi[sa][:, kc, c * P:(c + 1) * P], h_m[:, kc, :], start=(kc == 0), stop=(kc == NCH - 1)) for kc in range(NCH)],
                   reads=[B_wi[sa], B_h], writes=[pba])
                pa.append((psa, pba))
            for c in range(4):
                psa, pba = pa[c]
                psg, pbg = newbank()
                op("pe", [lambda e, kc=kc, c=c, psg=psg, sgt=sgt: e.matmul(psg, wi[sgt][:, kc, c * P:(c + 1) * P], h_m[:, kc, :], start=(kc == 0), stop=(kc == NCH - 1)) for kc in range(NCH)],
                   reads=[B_wi[sgt], B_h], writes=[pbg])
                op("act", lambda e, psg=psg: e.activation(out=sgm[:], in_=psg, func=AF.Sigmoid), reads=[pbg], writes=[B_rden[0]])
                op("dve", lambda e, psa=psa, c=c: e.tensor_tensor(out=zb[:, c, 30:30 + TM], in0=psa, in1=sgm[:], op=ALU.mult),
                   reads=[pba, B_rden[0]], writes=[B_z[c]])
            act_prefetch(AF.Ln)

            bank_set[0] = list(range(6))
            ps1, pb1 = psum[:, 6, :], banks[6]
            ps2, pb2 = psum[:, 7, :], banks[7]

            ATT_B = [0, 1, 2, 3]
            CONV_B = [4, 5]
            conv_state = {}

            def conv_taps(c, ja, jb):
                if ja == 0:
                    conv_state[c] = newbank(CONV_B)
                psc, pbc = conv_state[c]
                j0 = ja
                while j0 < jb:
                    n = min(4, jb - j0)
                    sl = dg_ctr[0] % 3
                    dg_ctr[0] += 1
                    op("dve", lambda e, c=c, j0=j0, n=n, sl=sl: e.tensor_tensor(
                        out=dg[:, sl * 4:sl * 4 + n, :], in0=ident_b[:].unsqueeze(1).broadcast_to([P, n, P]),
                        in1=cwb[:, c, j0:j0 + n].unsqueeze(2).broadcast_to([P, n, P]), op=ALU.mult),
                       reads=[B_cw, B_idb], writes=[B_dg[sl]])
                    op("pe", [lambda e, c=c, j=j, sl=sl, j0=j0, psc=psc: e.matmul(psc, dg[:, sl * 4 + j - j0, :], zb[:, c, j:j + TM], start=(j == 0), stop=(j == CW - 1)) for j in range(j0, j0 + n)],
                       reads=[B_dg[sl], B_z[c]], writes=[pbc])
                    j0 += n

            def conv_evac(c):
                psc, pbc = conv_state[c]
                if True:
                    op("act", lambda e, c=c, psc=psc: e.activation(out=acc[:, c, :], in_=psc, func=AF.Identity, bias=cwt[:, c, 31:32]),
                       reads=[pbc, B_cw], writes=[B_acc[c], B_h])
                    op("act", lambda e, c=c, psc=psc: e.activation(out=sq[0][:], in_=psc, func=AF.Identity, bias=cwt[:, c, 31:32]), reads=[pbc, B_cw], writes=[B_sq[0]])
                    op("act", lambda e, c=c, psc=psc: e.activation(out=sq[1][:], in_=psc, func=AF.Square, bias=cwt[:, c, 31:32]), reads=[pbc, B_cw], writes=[B_sq[1]])

            def stats_mm(c):
                op("pe", [lambda e, c=c: e.matmul(ps1, ones_b[:], sq[0][:], start=(c == 0), stop=(c == 3)),
                          lambda e, c=c: e.matmul(ps2, ones_b[:], sq[1][:], start=(c == 0), stop=(c == 3))],
                   reads=[B_sq[0], B_sq[1], B_ones], writes=[pb1, pb2])

            def attn_scores(b, kv):
                nblk = t * 4 + b
                kbs = [1] if nblk == 0 else [0, 1]
                pts = []
                for kb in kbs:
                    slot = b + kb
                    ps, pb = newbank(ATT_B)
                    op("pe", [lambda e, kb=kb, kv=kv, ps=ps: e.matmul(ps, ident_b[:], bias_sb[:, kb * 2 + kv, :], start=True, stop=False),
                              lambda e, kv=kv, slot=slot, ps=ps, b=b: e.matmul(ps[:, 0:256], kdT[:, 2 * kv, slot * P:(slot + 1) * P], qT[:, 2 * kv:2 * kv + 2, b * P:(b + 1) * P], start=False, stop=False),
                              lambda e, kv=kv, slot=slot, ps=ps, b=b: e.matmul(ps[:, 256:512], kdT[:, 2 * kv + 1, slot * P:(slot + 1) * P], qT[:, 2 * kv:2 * kv + 2, b * P:(b + 1) * P], start=False, stop=True)],
                       reads=[B_idb, B_bias, B_kd, B_q], writes=[pb])
                    pi = pt_ctr[0] % 4
                    pt_ctr[0] += 1
                    op("act", lambda e, ps=ps, pi=pi: e.activation(out=pT[pi][:], in_=ps, func=AF.Exp, scale=0.125), reads=[pb], writes=[B_pT[pi]])
                    pts.append((pi, slot))
                return pts

            def attn_pv(b, kv, pts):
                psd, pbd = newbank(ATT_B)
                pso, pbo = newbank(ATT_B)
                fns = []
                npt = len(pts)
                for i, (pi, slot) in enumerate(pts):
                    fns.append(lambda e, pi=pi, i=i, psd=psd, npt=npt: e.matmul(psd, ones_b[:], pT[pi][:], start=(i == 0), stop=(i == npt - 1)))
                    fns.append(lambda e, pi=pi, i=i, slot=slot, kv=kv, pso=pso, npt=npt: e.matmul(pso, Vd[:, slot, kv * P:(kv + 1) * P], pT[pi][:], start=(i == 0), stop=(i == npt - 1)))
                op("pe", fns, reads=[B_ones, B_vd] + [B_pT[pi] for pi, _ in pts], writes=[pbd, pbo])
                return (psd, pbd, pso, pbo)

            def attn_norm(b, kv, outs):
                psd, pbd, pso, pbo = outs
                rd = rden[kv]
                for i in range(4):
                    hh = l * NH + 4 * kv + GPERM[i]
                    op("dve", lambda e, i=i, hh=hh, psd=psd, rd=rd: e.tensor_scalar(out=rd[:, i * P:(i + 1) * P], in0=psd[:, i * P:(i + 1) * P], scalar1=esink[:, hh:hh + 1], scalar2=None, op0=ALU.add),
                       reads=[pbd, B_esink], writes=[B_rden[kv]])
                op("act", lambda e, rd=rd: e.activation(out=rd[:], in_=rd[:], func=AF.Ln), reads=[B_rden[kv]], writes=[B_rden[kv]])
                op("act", lambda e, rd=rd: e.activation(out=rd[:], in_=rd[:], func=AF.Exp, scale=-1.0), reads=[B_rden[kv]], writes=[B_rden[kv]])
                for half in range(2):
                    pr = slice(half * HD, (half + 1) * HD)
                    cs = slice(half * 256, (half + 1) * 256)
                    op("dve", lambda e, pr=pr, cs=cs, kv=kv, b=b, pso=pso, rd=rd: e.tensor_tensor(
                        out=mixT[pr, 2 * kv:2 * kv + 2, b * P:(b + 1) * P], in0=pso[pr, cs].rearrange("p (a q) -> p a q", a=2),
                        in1=rd[pr, cs].rearrange("p (a q) -> p a q", a=2), op=ALU.mult),
                       reads=[pbo, B_rden[kv]], writes=[B_mixa[kv]])

            for c in range(4):
                conv_taps(c, 0, CW)
                if c > 0:
                    stats_mm(c - 1)
                conv_evac(c)
            stats_mm(3)
            op("dve", lambda e, ps1=ps1: e.tensor_scalar(out=mean_t[:], in0=ps1, scalar1=1.0 / 512, scalar2=None, op0=ALU.mult), reads=[pb1], writes=[B_lnt])
            op("dve", lambda e: e.tensor_tensor(out=rstd2[:], in0=mean_t[:], in1=mean_t[:], op=ALU.mult), reads=[B_lnt], writes=[B_rstd])
            op("dve", lambda e, ps2=ps2: e.scalar_tensor_tensor(out=rstd2[:], in0=ps2, scalar=1.0 / 512, in1=rstd2[:], op0=ALU.mult, op1=ALU.subtract), reads=[pb2, B_rstd], writes=[B_rstd])
            ATT_B[:] = list(range(8))
            units = [(u // 2, u % 2) for u in range(8)]
            pts_next = attn_scores(*units[0])
            for u in range(8):
                bb, kv = units[u]
                pts = pts_next
                if u + 1 < 8:
                    pts_next = attn_scores(*units[u + 1])
                outs = attn_pv(bb, kv, pts)
                attn_norm(bb, kv, outs)
            bank_set[0] = list(range(8))
            resid_multi(t0, [(so_att[r], (lambda jj, r=r: mixT[:, 2 * r + jj, :]), [B_mixa[r]]) for r in range(2)], 1.0)
            so_cv = [load_wo(wout_d[l, 256 * r:256 * r + 256, :]) for r in range(2, 4)]
            op("act", lambda e: e.activation(out=rstd2[:], in_=rstd2[:], func=AF.Ln, bias=eps_t[:]), reads=[B_rstd, B_eps], writes=[B_rstd])
            op("act", lambda e: e.activation(out=rstd2[:], in_=rstd2[:], func=AF.Exp, scale=-0.5), reads=[B_rstd], writes=[B_rstd])
            act_prefetch(AF.Silu)
            for c in range(4):
                op("dve", lambda e, c=c: e.tensor_tensor(out=acc[:, c, :], in0=acc[:, c, :], in1=mean_t[:], op=ALU.subtract), reads=[B_acc[c], B_lnt], writes=[B_acc[c]])
                op("dve", lambda e, c=c: e.tensor_tensor(out=acc[:, c, :], in0=acc[:, c, :], in1=rstd2[:], op=ALU.mult), reads=[B_acc[c], B_rstd], writes=[B_acc[c]])
                op("act", lambda e, c=c: e.activation(out=mixT[:, 4 + c, :], in_=acc[:, c, :], func=AF.Silu, scale=cwt[:, c, 32:33], bias=cwt[:, c, 33:34]),
                   reads=[B_acc[c], B_cw, B_h], writes=[B_mixc[c]])
            act_prefetch(AF.Ln)
            resid_multi(t0, [(so_cv[r - 2], (lambda jj, r=r: mixT[:, 2 * r + jj, :]), [B_mixc[2 * (r - 2)], B_mixc[2 * (r - 2) + 1]]) for r in range(2, 4)], 1.0)

    op("pool", lambda e: e.memset(ident_f[:], 0.0), writes=[B_ident])
    op("pool", lambda e: e.affine_select(out=ident_f[:], in_=ident_f[:], pattern=[[-1, P]], compare_op=ALU.not_equal, fill=1.0, base=0, channel_multiplier=1),
       reads=[B_ident], writes=[B_ident])
    op("pool", lambda e: e.tensor_copy(out=ident_b[:], in_=ident_f[:]), reads=[B_ident], writes=[B_idb])
    op("pool", lambda e: e.memset(ones_b[:], 1.0), writes=[B_ones])
    op("pool", lambda e: e.memset(eps_t[:], EPS), writes=[B_eps])
    NG = 3 * DEPTH + 1
    op("sp", [lambda e: e.dma_start(out=gstage[0:DEPTH, :], in_=nf1_d),
              lambda e: e.dma_start(out=gstage[DEPTH:2 * DEPTH, :], in_=nmx_d),
              lambda e: e.dma_start(out=gstage[2 * DEPTH:3 * DEPTH, :], in_=nf2_d),
              lambda e: e.dma_start(out=gstage[3 * DEPTH:NG, :], in_=fn_d.rearrange("(o n) -> o n", o=1)),
              lambda e: e.dma_start(out=sink_sb[:], in_=sinks_d.rearrange("a b -> (a b)").partition_broadcast(P)),
              lambda e: e.dma_start(out=bias_st[:], in_=bias8_d.rearrange("a b p n -> p (a b) n"))],
       writes=[B_gst], dma_sem="d_misc")
    for c in range(NCH):
        ps, pb = newbank()
        op("pe", lambda e, c=c, ps=ps: e.transpose(ps[:, 0:NG], gstage[0:NG, c * P:(c + 1) * P], ident_f[0:NG, 0:NG]), reads=[B_gst, B_ident], writes=[pb])
        op("dve", lambda e, c=c, ps=ps: e.tensor_copy(out=gT[:, c, 0:NG], in_=ps[:, 0:NG]), reads=[pb], writes=[B_gT])
    op("act", lambda e: e.activation(out=esink[:], in_=sink_sb[:], func=AF.Exp), reads=[B_gst], writes=[B_esink])
    op("dve", lambda e: e.tensor_copy(out=bias_sb[:], in_=bias_st[:]), reads=[B_gst], writes=[B_bias])
    for blk in range(NBLK):
        s = blk % 2
        op("sp", [lambda e, s=s, blk=blk: e.dma_start(out=stage[s][:], in_=x_d[blk * P:(blk + 1) * P, :])], writes=[B_stage[s]], dma_sem="d_st%d" % s)
        for half in range(2):
            ps, pb = newbank()
            op("pe", [lambda e, s=s, c=c, ps=ps, half=half: e.transpose(ps[:, c * P:(c + 1) * P], stage[s][:, (half * 4 + c) * P:(half * 4 + c + 1) * P], ident_f[:]) for c in range(4)],
               reads=[B_stage[s], B_ident], writes=[pb])
            xs = blk * P // SUB
            dst = xT[:, half * 4:half * 4 + 4, blk * P:(blk + 1) * P]
            src = ps.rearrange("p (c q) -> p c q", c=4)
            if half == 0:
                op("act", lambda e, dst=dst, src=src: e.copy(out=dst, in_=src), reads=[pb], writes=[B_x[half * 4 + c][xs] for c in range(4)])
            else:
                op("dve", lambda e, dst=dst, src=src: e.tensor_copy(out=dst, in_=src), reads=[pb], writes=[B_x[half * 4 + c][xs] for c in range(4)])

    import os
    stages = os.environ.get("KSTAGES", "f1,mix,f2").split(",")
    for l in range(DEPTH):
        if "f1" in stages:
            if l == 0 or "mix" not in stages or "f2" not in stages:
                pg.fence()
            ffn_phase(l, w1i_d, w1o_d, 0 * DEPTH + l)
        if "mix" in stages:
            pg.fence()
            mixer_phase(l)
        if "f2" in stages:
            pg.fence()
            ffn_phase(l, w2i_d, w2o_d, 2 * DEPTH + l)

    pg.fence()
    last_tok = [None, None]
    oi = 0
    for t in range(S // SUB):
        t0 = t * SUB
        rmsnorm(t0, SUB, 3 * DEPTH, None, rstd_o, lnt_o, out_f32=yT)
        for b in range(4):
            s = oi % 2
            oi += 1
            for half in range(2):
                ps, pb = newbank()
                op("pe", [lambda e, c=c, ps=ps, half=half, b=b: e.transpose(ps[:, c * P:(c + 1) * P], yT[:, half * 4 + c, b * P:(b + 1) * P], ident_f[:]) for c in range(4)],
                   reads=[B_yT, B_ident], writes=[pb])
                if half == 0:
                    op("act", lambda e, s=s, ps=ps: e.copy(out=stage[s][:, 0:512], in_=ps), reads=[pb], writes=[B_stage[s]])
                else:
                    op("dve", lambda e, s=s, ps=ps: e.tensor_copy(out=stage[s][:, 512:1024], in_=ps), reads=[pb, B_stage[s]], writes=[B_stage[s]])
            r0 = t0 + b * P
            last_tok[s] = op("sp", [lambda e, s=s, r0=r0: e.dma_start(out=out_d[r0:r0 + P, :], in_=stage[s][:])], reads=[B_stage[s]], dma_sem="d_out%d" % s)
    pg.final_wait("sp", [tk for tk in last_tok if tk is not None])
    pg.emit()
    es.close()
    return nc


def make_bias8():
    out = np.zeros((2, 2, P, 4, P), np.float32)
    s = np.arange(P)[:, None]
    q = np.arange(P)[None, :]
    for kb in range(2):
        dist = (q - s + P) if kb == 0 else (q - s)
        valid = (dist >= 0) & (dist < P)
        for kv in range(2):
            for i in range(4):
                hh = 4 * kv + GPERM[i]
                slope = 2.0 ** (-(hh + 1))
                out[kb, kv, :, i, :] = np.where(valid, -8.0 * slope * dist, -240000.0)
    return out.reshape(2, 2, P, 512)


_NAMES = ["norm_ffn1", "w_ffn1_in", "w_ffn1_out", "norm_mix", "w_in", "sinks", "w_dw", "b_dw",
          "conv_ln_g", "conv_ln_b", "w_out", "norm_ffn2", "w_ffn2_in", "w_ffn2_out", "final_norm"]


def kernel(**inputs):
    x = np.ascontiguousarray(inputs["x"], dtype=np.float32)
    B, S, _ = x.shape
    DEPTH = inputs["w_in"].shape[0]
    nc = build_program(DEPTH, S)
    shared = {n: np.ascontiguousarray(inputs[n], dtype=np.float32) for n in _NAMES}
    shared["bias8"] = make_bias8()
    in_maps = [dict(shared, x=x[b]) for b in range(B)]
    res = run_bass_kernel_spmd(nc, in_maps, core_ids=list(range(B)))
    return np.stack([np.asarray(r["out"], dtype=np.float32) for r in res.results], axis=0)
```
